# Optimizing a Trainium2 kernel written in Bass

```python
import jax, jax.numpy as jnp
from jax import lax
import numpy as np

D_MODEL = 2048
BATCH = 1
SEQ = 16384
DEPTH = 2

CHUNK = 64
HEAD_DIM = 128
D_MIX = D_MODEL
D_GDN = D_MIX // 2
D_SB = D_MIX - D_GDN
N_HEADS_GDN = D_GDN // HEAD_DIM
N_HEADS_SB = D_SB // HEAD_DIM
SHORT_CONV = 4
FFN_CONV = 3
D_FF = ((8 * D_MODEL // 3 + 255) // 256) * 256
SB_BLOCK = 128
EPS = 1e-6
D_IN = 4 * D_GDN + 2 * N_HEADS_GDN + 3 * D_SB

kernel_name = "hybrid_gdn_stickbreaking_convffn_adaln"


def rms_norm(x, gain):
    xf = x.astype(jnp.float32)
    y = xf * lax.rsqrt(jnp.mean(xf * xf, axis=-1, keepdims=True) + EPS)
    return (y * gain.astype(jnp.float32)).astype(x.dtype)


def l2_norm(x):
    xf = x.astype(jnp.float32)
    return xf * lax.rsqrt(jnp.sum(xf * xf, axis=-1, keepdims=True) + EPS)


def causal_dwconv(x, w):
    width = w.shape[0]
    t = x.shape[1]
    xp = jnp.pad(x, ((0, 0), (width - 1, 0), (0, 0)))
    y = xp[:, width - 1:width - 1 + t] * w[width - 1]
    for i in range(width - 1):
        y = y + xp[:, i:i + t] * w[i]
    return y


def gated_delta_rule(q, k, v, g, beta):
    b, t, h, dk = q.shape
    dv = v.shape[-1]
    n, c = t // CHUNK, CHUNK

    def chunks(a):
        a = a.reshape((b, n, c, h) + a.shape[3:])
        return jnp.moveaxis(a, (1, 3), (0, 2))

    q, k, v, g, beta = chunks(q), chunks(k), chunks(v), chunks(g), chunks(beta)
    g = jnp.cumsum(g, axis=-1)
    tri_incl = jnp.tril(jnp.ones((c, c), bool))
    tri_strict = jnp.tril(jnp.ones((c, c), bool), -1)
    decay = jnp.exp(jnp.where(tri_incl, g[..., :, None] - g[..., None, :], -jnp.inf))
    k_beta = k * beta[..., None]
    v_beta = v * beta[..., None]
    lower = jnp.where(tri_strict, jnp.einsum('nbhik,nbhjk->nbhij', k_beta, k) * decay, 0.0)
    eye = jnp.eye(c, dtype=q.dtype)
    t_inv = lax.linalg.triangular_solve(eye + lower, jnp.broadcast_to(eye, lower.shape),
                                        left_side=True, lower=True)
    u = t_inv @ v_beta
    w = t_inv @ (k_beta * jnp.exp(g)[..., None])
    attn_intra = jnp.where(tri_incl, jnp.einsum('nbhik,nbhjk->nbhij', q, k) * decay, 0.0)

    def step(state, inp):
        q_c, k_c, u_c, w_c, g_c, a_c = inp
        v_new = u_c - w_c @ state
        o = (q_c * jnp.exp(g_c)[..., None]) @ state + a_c @ v_new
        g_last = g_c[..., -1]
        k_dec = k_c * jnp.exp(g_last[..., None] - g_c)[..., None]
        state = state * jnp.exp(g_last)[..., None, None] + jnp.einsum('bhck,bhcv->bhkv', k_dec, v_new)
        return state, o

    s0 = jnp.zeros((b, h, dk, dv), q.dtype)
    _, o = lax.scan(step, s0, (q, k, u, w, g, attn_intra))
    return jnp.moveaxis(o, (0, 2), (1, 3)).reshape(b, t, h, dv)


def stick_breaking_attention(q, k, v):
    b, t, h, d = q.shape
    nb = t // SB_BLOCK
    kh = jnp.swapaxes(k, 1, 2)
    vh = jnp.swapaxes(v, 1, 2)
    qb = q.reshape(b, nb, SB_BLOCK, h, d).transpose(1, 0, 3, 2, 4)
    key_pos = jnp.arange(t)
    scale = d ** -0.5

    def block(args):
        q_blk, start = args
        q_pos = start + jnp.arange(SB_BLOCK)
        mask = key_pos[None, :] < q_pos[:, None]
        z = jnp.einsum('bhqd,bhkd->bhqk', q_blk, kh) * scale
        log_stay = jnp.where(mask, jax.nn.log_sigmoid(-z), 0.0)
        between = lax.cumsum(log_stay, axis=3, reverse=True) - log_stay
        a = jnp.where(mask, jnp.exp(jax.nn.log_sigmoid(z) + between), 0.0)
        return jnp.einsum('bhqk,bhkd->bhqd', a, vh)

    starts = jnp.arange(nb) * SB_BLOCK
    o = lax.map(block, (qb, starts))
    return o.transpose(1, 0, 3, 2, 4).reshape(b, t, h, d)


def hybrid_layer(x, mod, norm1, w_in, conv_qkv, a_log, dt_bias, gdn_norm,
                 sb_q_norm, sb_k_norm, w_out, norm2, w_up, conv_ffn, w_down):
    b, t, _ = x.shape
    shift1, scale1, gate1, shift2, scale2, gate2 = jnp.split(mod, 6, axis=-1)

    h = rms_norm(x, norm1) * (1.0 + scale1[:, None]) + shift1[:, None]
    proj = h @ w_in
    o0 = 3 * D_GDN
    o1 = o0 + D_GDN
    o2 = o1 + N_HEADS_GDN
    o3 = o2 + N_HEADS_GDN
    qkv_a = jax.nn.silu(causal_dwconv(proj[..., :o0], conv_qkv)).astype(jnp.float32)
    z_a = proj[..., o0:o1].astype(jnp.float32)
    b_a = proj[..., o1:o2].astype(jnp.float32)
    a_a = proj[..., o2:o3].astype(jnp.float32)
    q_b, k_b, v_b = jnp.split(proj[..., o3:].astype(jnp.float32), 3, axis=-1)

    q_a, k_a, v_a = jnp.split(qkv_a, 3, axis=-1)
    q_a = l2_norm(q_a.reshape(b, t, N_HEADS_GDN, HEAD_DIM)) * (HEAD_DIM ** -0.5)
    k_a = l2_norm(k_a.reshape(b, t, N_HEADS_GDN, HEAD_DIM))
    v_a = v_a.reshape(b, t, N_HEADS_GDN, HEAD_DIM)
    beta = jax.nn.sigmoid(b_a)
    g = -jnp.exp(a_log.astype(jnp.float32)) * jax.nn.softplus(a_a + dt_bias.astype(jnp.float32))
    o_a = gated_delta_rule(q_a, k_a, v_a, g, beta)
    o_a = rms_norm(o_a, gdn_norm) * jax.nn.silu(z_a.reshape(b, t, N_HEADS_GDN, HEAD_DIM))
    o_a = o_a.reshape(b, t, D_GDN)

    q_b = rms_norm(q_b.reshape(b, t, N_HEADS_SB, HEAD_DIM), sb_q_norm)
    k_b = rms_norm(k_b.reshape(b, t, N_HEADS_SB, HEAD_DIM), sb_k_norm)
    v_b = v_b.reshape(b, t, N_HEADS_SB, HEAD_DIM)
    o_b = stick_breaking_attention(q_b, k_b, v_b).reshape(b, t, D_SB)

    y = jnp.concatenate([o_a, o_b], axis=-1).astype(x.dtype) @ w_out
    x = x + gate1[:, None] * y

    h2 = rms_norm(x, norm2) * (1.0 + scale2[:, None]) + shift2[:, None]
    gate_br, val_br = jnp.split(h2 @ w_up, 2, axis=-1)
    f = (jax.nn.silu(causal_dwconv(gate_br, conv_ffn)) * val_br) @ w_down
    return x + gate2[:, None] * f


def setup_inputs(seed: int = 0) -> dict:
    key = jax.random.key(seed)
    ks = jax.random.split(key, 20)
    f32 = jnp.float32
    nrm = lambda k, shape, s: jax.random.normal(k, shape, f32) * s
    dt = jnp.exp(jax.random.uniform(ks[8], (DEPTH, N_HEADS_GDN), f32, np.log(1e-3), np.log(1e-1)))
    return {
        "x": nrm(ks[0], (BATCH, SEQ, D_MODEL), 1.0),
        "c": nrm(ks[1], (BATCH, D_MODEL), 1.0),
        "w_ada": nrm(ks[2], (DEPTH, D_MODEL, 6 * D_MODEL), 0.5 * D_MODEL ** -0.5),
        "b_ada": nrm(ks[3], (DEPTH, 6 * D_MODEL), 0.01),
        "norm1": 1.0 + nrm(ks[4], (DEPTH, D_MODEL), 0.01),
        "w_in": nrm(ks[5], (DEPTH, D_MODEL, D_IN), D_MODEL ** -0.5),
        "conv_qkv": nrm(ks[6], (DEPTH, SHORT_CONV, 3 * D_GDN), SHORT_CONV ** -0.5),
        "a_log": jnp.log(jax.random.uniform(ks[7], (DEPTH, N_HEADS_GDN), f32, 1.0, 16.0)),
        "dt_bias": dt + jnp.log(-jnp.expm1(-dt)),
        "gdn_norm": 1.0 + nrm(ks[9], (DEPTH, HEAD_DIM), 0.01),
        "sb_q_norm": 1.0 + nrm(ks[10], (DEPTH, HEAD_DIM), 0.01),
        "sb_k_norm": 1.0 + nrm(ks[11], (DEPTH, HEAD_DIM), 0.01),
        "w_out": nrm(ks[12], (DEPTH, D_MIX, D_MODEL), D_MIX ** -0.5),
        "norm2": 1.0 + nrm(ks[13], (DEPTH, D_MODEL), 0.01),
        "w_up": nrm(ks[14], (DEPTH, D_MODEL, 2 * D_FF), D_MODEL ** -0.5),
        "conv_ffn": nrm(ks[15], (DEPTH, FFN_CONV, D_FF), FFN_CONV ** -0.5),
        "w_down": nrm(ks[16], (DEPTH, D_FF, D_MODEL), D_FF ** -0.5),
    }


def reference(x, c, w_ada, b_ada, norm1, w_in, conv_qkv, a_log, dt_bias, gdn_norm,
              sb_q_norm, sb_k_norm, w_out, norm2, w_up, conv_ffn, w_down):
    c_act = jax.nn.silu(c)
    for l in range(DEPTH):
        mod = c_act @ w_ada[l] + b_ada[l]
        x = hybrid_layer(x, mod, norm1[l], w_in[l], conv_qkv[l], a_log[l], dt_bias[l],
                         gdn_norm[l], sb_q_norm[l], sb_k_norm[l], w_out[l], norm2[l],
                         w_up[l], conv_ffn[l], w_down[l])
    return x
```

```python
import contextlib
import numpy as np
import ml_dtypes
import concourse.bass as bass
import concourse.mybir as mybir
from concourse.bass_utils import run_bass_kernel_spmd

F32 = mybir.dt.float32
BF16 = mybir.dt.bfloat16
AF = mybir.ActivationFunctionType
ALU = mybir.AluOpType

D = 2048
DFF = 5632
NCORES = 8
EPS = 1e-6
ENGS = ("pe", "act", "dve", "pool", "sp")
SEM_LIMIT = 6000
DSEM_LIMIT = 6000


class V:
    __slots__ = ("t", "ap")

    def __init__(self, t, ap):
        self.t = t
        self.ap = ap

    def __getitem__(self, idx):
        return V(self.t, self.ap[idx])


class T:
    __slots__ = ("ap", "name", "w", "r", "dsem", "dcnt", "psum", "bank", "br")

    def __init__(self, ap, name, psum=False, bank=None):
        self.ap = ap
        self.name = name
        self.psum = psum
        self.bank = bank if bank is not None else self
        self.br = {}
        self.w = None
        self.r = {}
        self.dsem = None
        self.dcnt = 0

    def __getitem__(self, idx):
        return V(self, self.ap[idx])


class Prog:
    def __init__(self, nc):
        self.nc = nc
        self.stack = contextlib.ExitStack()
        self.ops = {e: [] for e in ENGS}
        self.esem = {}
        self.cnt = {e: 0 for e in ENGS}
        self.known = {e: {} for e in ENGS}
        self.nsem = 0
        self.ntile = 0
        self.dumps = []
        self.pstack = None
        self.ptiles = []
        self.pid_cache = {}
        self.free_dsems = []
        self.live_dsems = []
        self.epoch = {e: 0 for e in ENGS}
        self.ecnt = {e: 0 for e in ENGS}
        for e in ENGS:
            self.esem[e] = self.stack.enter_context(nc.semaphore("s_" + e))

    def sb(self, shape, dt, name=None):
        self.ntile += 1
        name = (name or "t") + f"_{self.ntile}"
        t = self.pstack.enter_context(self.nc.sbuf_tensor(name, list(shape), dt))
        tt_ = T(t, name)
        self.ptiles.append(tt_)
        return tt_

    def ps(self, shape, dt, name=None):
        self.ntile += 1
        name = (name or "p") + f"_{self.ntile}"
        t = self.pstack.enter_context(self.nc.psum_tensor(name, list(shape), dt))
        return T(t, name, psum=True)

    def phase_begin(self):
        self.pstack = contextlib.ExitStack()
        self.ptiles = []
        self.pid_cache = {}

    def _new_dsem(self, t):
        if self.free_dsems:
            t.dsem, t.dcnt = self.free_dsems.pop()
        else:
            self.nsem += 1
            t.dsem = self.stack.enter_context(self.nc.semaphore(f"d{self.nsem}"))
            t.dcnt = 0
        self.live_dsems.append(t)

    def phase_end(self):
        evs = []
        for e in ENGS:
            if self.ecnt[e] > 0:
                evs.append(((e, self.epoch[e]), self.esem[e], self.ecnt[e]))
        for t in self.live_dsems:
            if t.dsem is not None and t.dcnt > 0:
                evs.append((("d", id(t), id(t.dsem)), t.dsem, t.dcnt))
        for e in ENGS:
            waits = []
            kn = self.known[e]
            for (k, sm, v) in evs:
                if k[0] == e:
                    continue
                if kn.get(k, 0) < v:
                    kn[k] = v
                    waits.append((sm, v))

            def run(eh, waits=waits):
                for (sm, val) in waits:
                    eh.wait_ge(sm, val)
            self.ops[e].append(run)
        self._emit_block()
        for e in ENGS:
            self.ops[e] = []
        dead = set(id(t) for t in self.ptiles)
        keep = []
        for t in self.live_dsems:
            if id(t) in dead:
                if t.dsem is not None and t.dcnt < DSEM_LIMIT // 2:
                    self.free_dsems.append((t.dsem, t.dcnt))
            else:
                keep.append(t)
        self.live_dsems = keep
        self.pstack.close()
        self.pstack = None

    def pid(self, eng_name, e):
        if eng_name not in self.pid_cache:
            self.pid_cache[eng_name] = e.partition_id()
        return self.pid_cache[eng_name]

    def view(self, v, name="v"):
        return T(v.ap, name, psum=v.t.psum, bank=v.t.bank)

    def dram(self, ap, name="d"):
        return T(ap, name)

    def _waits(self, eng, reads, writes):
        need = {}

        def add(ev):
            if ev is None:
                return
            k, s, v = ev
            if eng == "pe" and k[0] == "pe":
                return
            if k not in need or need[k][1] < v:
                need[k] = (s, v)
        for t in reads:
            add(t.w)
            if t.psum:
                for k, (s, v) in t.bank.br.items():
                    if k[0] != eng:
                        add((k, s, v))
        for t in writes:
            add(t.w)
            for k, (s, v) in t.r.items():
                add((k, s, v))
        out = []
        kn = self.known[eng]
        for k, (s, v) in need.items():
            if kn.get(k, 0) < v:
                kn[k] = v
                out.append((s, v))
        return out

    def _mark(self, ev, reads, writes):
        k, s, v = ev
        for t in reads:
            t.r[k] = (s, v)
            if t.psum:
                t.bank.br[k] = (s, v)
        for t in writes:
            t.w = ev
            t.r = {}

    def op(self, eng, fn, reads=(), writes=()):
        self.total = getattr(self, "total", 0) + 1
        if self.total > getattr(self, "maxops", 10 ** 9):
            return
        waits = self._waits(eng, reads, writes)
        if self.ecnt[eng] >= SEM_LIMIT:
            self.epoch[eng] += 1
            self.ecnt[eng] = 0
            self.esem[eng] = self.stack.enter_context(self.nc.semaphore(f"s_{eng}{self.epoch[eng]}"))
        self.ecnt[eng] += 1
        v = self.ecnt[eng]
        sem = self.esem[eng]

        def run(e, waits=waits, fn=fn, sem=sem):
            for (s, val) in waits:
                e.wait_ge(s, val)
            fn(e).then_inc(sem, 1)
        self.ops[eng].append(run)
        self._mark(((eng, self.epoch[eng]), sem, v), reads, writes)

    def dma(self, q, out, in_, **kw):
        out_t = out.t
        reads = [in_.t]
        writes = [out_t]
        waits = self._waits(q, reads, writes)
        if out_t.dsem is None or out_t.dcnt + 16 > DSEM_LIMIT:
            if out_t in self.live_dsems:
                self.live_dsems.remove(out_t)
            self._new_dsem(out_t)
        out_t.dcnt += 16
        sem, v = out_t.dsem, out_t.dcnt
        oap, iap = out.ap, in_.ap

        def run(e, waits=waits):
            for (s, val) in waits:
                e.wait_ge(s, val)
            o_ = oap(self.pid(q, e)) if callable(oap) else oap
            i_ = iap(self.pid(q, e)) if callable(iap) else iap
            try:
                e.dma_start(out=o_, in_=i_, **kw).then_inc(sem, 16)
            except Exception:
                print("DMA build failed: out", o_, " in", i_, kw)
                raise
        self.ops[q].append(run)
        self._mark((("d", id(out_t), self.nsem if False else id(sem)), sem, v), reads, writes)

    def dump(self, name, v, shape, dt):
        ap = self.nc.dram_tensor("dbg_" + name, list(shape), dt, kind="ExternalOutput").ap()
        t = self.dram(ap, "dbg_" + name)
        idx = tuple(slice(None) for _ in shape)
        self.dma("pool", V(t, ap[idx]), v)
        self.dumps.append(t)

    def coll(self, kind, out, in_):
        out_t = out.t
        reads, writes = [in_.t], [out_t]
        waits = self._waits("pool", reads, writes)
        if out_t.dsem is None:
            self.nsem += 1
            out_t.dsem = self.stack.enter_context(self.nc.semaphore(f"c{self.nsem}"))
            out_t.dcnt = 0
            self.live_dsems.append(out_t)
        out_t.dcnt += 1
        sem, v = out_t.dsem, out_t.dcnt
        oap, iap = out.ap, in_.ap

        def run(e, waits=waits):
            for (s, val) in waits:
                e.wait_ge(s, val)
            e.collective_compute(kind, ALU.bypass, replica_groups=[list(range(NCORES))],
                                 ins=[iap], outs=[oap]).then_inc(sem, 1)
        self.ops["pool"].append(run)
        self._mark((("d", id(out_t), id(sem)), sem, v), reads, writes)

    def wait_all(self, eng, tiles):
        waits = self._waits(eng, tiles, ())

        def run(e, waits=waits):
            for (s, val) in waits:
                e.wait_ge(s, val)
        self.ops[eng].append(run)

    def mm(self, out, pairs, start=True, stop=True):
        reads = [x.t for p in pairs for x in p]
        n = len(pairs)

        def fn(e):
            ins = None
            for i, (l, r) in enumerate(pairs):
                ins = e.matmul(out.ap, l.ap, r.ap, start=(start and i == 0), stop=(stop and i == n - 1))
            return ins
        self.op("pe", fn, reads, [out.t])

    def act(self, out, in_, func, bias=None, scale=None, eng="act"):
        reads = [in_.t]
        kw = {}
        if bias is not None:
            if isinstance(bias, V):
                reads.append(bias.t)
                kw["bias"] = bias.ap
            else:
                kw["bias"] = float(bias)
        if scale is not None:
            if isinstance(scale, V):
                reads.append(scale.t)
                kw["scale"] = scale.ap
            else:
                kw["scale"] = float(scale)
        self.op("act", lambda e: e.activation(out.ap, in_.ap, func, **kw), reads, [out.t])

    def tt(self, out, a, b, op, eng="dve"):
        self.op(eng, lambda e: e.tensor_tensor(out.ap, a.ap, b.ap, op), [a.t, b.t], [out.t])

    def ts(self, out, a, s1, s2, op0, op1=None, eng="dve"):
        reads = [a.t]
        if isinstance(s1, V):
            reads.append(s1.t)
            s1 = s1.ap
        if isinstance(s2, V):
            reads.append(s2.t)
            s2 = s2.ap
        if op1 is None:
            self.op(eng, lambda e: e.tensor_single_scalar(out.ap, a.ap, s1, op0), reads, [out.t])
        else:
            self.op(eng, lambda e: e.tensor_scalar(out.ap, a.ap, s1, s2, op0, op1), reads, [out.t])

    def stt(self, out, a, s, b, op0, op1, eng="dve"):
        reads = [a.t, b.t]
        if isinstance(s, V):
            reads.append(s.t)
            s = s.ap
        self.op(eng, lambda e: e.scalar_tensor_tensor(out.ap, a.ap, s, b.ap, op0, op1), reads, [out.t])

    def copy(self, out, in_, eng="dve"):
        if eng == "act":
            self.op("act", lambda e: e.copy(out.ap, in_.ap), [in_.t], [out.t])
        else:
            self.op(eng, lambda e: e.tensor_copy(out.ap, in_.ap), [in_.t], [out.t])

    def finish(self):
        self.stack.close()

    def emit(self):
        self._emit_block()
        if self.pstack is not None:
            self.pstack.close()
        self.stack.close()

    def _emit_block(self):
        nc = self.nc
        ops = self.ops
        with nc.Block() as block:
            @block.tensor
            def _(e):
                for f in ops["pe"]:
                    f(e)

            @block.scalar
            def _(e):
                for f in ops["act"]:
                    f(e)

            @block.vector
            def _(e):
                for f in ops["dve"]:
                    f(e)

            @block.gpsimd
            def _(e):
                for f in ops["pool"]:
                    f(e)

            @block.sync
            def _(e):
                for f in ops["sp"]:
                    f(e)


class Rot:
    def __init__(self, tiles):
        self.tiles = tiles
        self.i = 0

    def get(self):
        t = self.tiles[self.i % len(self.tiles)]
        self.i += 1
        return t


NEG = -30000.0
C_IDN, C_ONE, C_UIN, C_BUI, C_BD, C_NLS, C_NUI, C_CI = 0, 128, 256, 384, 512, 640, 768, 896
C_W = 898


def make_consts():
    c = np.zeros((128, C_W), np.float32)
    i = np.arange(128)
    c[:, C_IDN:C_IDN + 128] = np.eye(128)
    c[:, C_ONE:C_ONE + 128] = 1.0
    c[:, C_UIN:C_UIN + 128] = (i[:, None] >= i[None, :])
    same = (i[:, None] // 64) == (i[None, :] // 64)
    c[:, C_BUI:C_BUI + 128] = same & (i[:, None] <= i[None, :])
    c[:, C_BD:C_BD + 128] = same
    c[:, C_NLS:C_NLS + 128] = np.where(same & (i[:, None] > i[None, :]), 0.0, NEG)
    c[:, C_NUI:C_NUI + 128] = np.where(same & (i[:, None] <= i[None, :]), 0.0, NEG)
    c[:, C_CI + 0] = (i < 64)
    c[:, C_CI + 1] = (i >= 64)
    return c


def make_sbmask():
    s = np.arange(128)[:, None]
    u = np.arange(896)[None, :]
    return (s < u - 384).astype(np.float32).astype(ml_dtypes.bfloat16)


def phase_A(P, nc, Tn, l, XS5, DXS, OLOC, DOL, cst, sbm, cvec, stage=99):
    NT = Tn // 512
    NB = Tn // 128
    TB = Tn // NCORES
    TW = min(512, TB)
    UL = min(256, TW)
    dT = lambda name, shape, dt, kind: nc.dram_tensor(f"{name}_{l}", list(shape), dt, kind=kind).ap()
    wada = dT("wada_a", [D, 4096], F32, "ExternalInput")
    bada = dT("bada_a", [128, 32], F32, "ExternalInput")
    n1 = dT("n1", [128, 16], F32, "ExternalInput")
    win = dT("win", [D, 898], F32, "ExternalInput")
    convw = dT("convw", [128, 12], F32, "ExternalInput")
    pvec = dT("pvec", [128, 8], F32, "ExternalInput")

    P.phase_begin()
    DX, DO = DXS, DOL
    CST = P.sb([128, C_W], F32, "cst")
    SBM = P.sb([128, 896], BF16, "sbm")
    PV = P.sb([128, 8], F32, "pv")
    CW = P.sb([128, 12], F32, "cw")
    CV_ = P.sb([128, 16], F32, "cvec")
    N1 = P.sb([128, 16], F32, "n1")
    BAD = P.sb([128, 32], F32, "bada")
    MODV = P.sb([128, 32], F32, "modv")
    G1 = P.sb([128, 16], F32, "g1")
    WB = P.sb([128, 16, 898], BF16, "wb")
    KT = P.sb([128, Tn], BF16, "kt")
    VA = P.sb([128, NB, 128], BF16, "va")
    P.dma("sp", CST[:], V(P.dram(cst), cst[:, :]))
    P.dma("sp", SBM[:], V(P.dram(sbm), sbm[:, :]))
    P.dma("sp", PV[:], V(P.dram(pvec), pvec[:, :]))
    P.dma("sp", CW[:], V(P.dram(convw), convw[:, :]))
    P.dma("sp", CV_[:], V(P.dram(cvec), cvec[:, :]))
    P.dma("sp", N1[:], V(P.dram(n1), n1[:, :]))
    P.dma("sp", BAD[:], V(P.dram(bada), bada[:, :]))
    win_r = win.rearrange("(kc p) n -> p kc n", p=128)
    DWIN = P.dram(win, "win")
    for g in range(4):
        P.dma("pool", WB[:, 4 * g:4 * g + 4, :], V(DWIN, win_r[:, 4 * g:4 * g + 4, :]))
    IDN = CST[:, C_IDN:C_IDN + 128]
    ONE = CST[:, C_ONE:C_ONE + 128]
    UIN = CST[:, C_UIN:C_UIN + 128]
    BUI = CST[:, C_BUI:C_BUI + 128]
    BDM = CST[:, C_BD:C_BD + 128]
    NLS = CST[:, C_NLS:C_NLS + 128]
    NUI = CST[:, C_NUI:C_NUI + 128]
    CI = CST[:, C_CI:C_CI + 2]

    big = Rot([P.ps([128, 512], F32, f"big{i}") for i in range(3)])
    AVP = P.ps([128, 512], F32, "avp")
    OTP = P.ps([128, 512], F32, "otp")
    smalls = [P.ps([128, 512], F32, f"smb{i}") for i in range(3)]
    smalls = [P.view(b[:, 0:128], f"sm{i}") for i, b in enumerate(smalls)]
    small = Rot(smalls)

    XR = Rot([P.sb([128, 16, 128], F32, f"x{i}") for i in range(2)])
    SQ = Rot([P.sb([128, 4, 128], F32, f"sq{i}") for i in range(2)])
    HT = P.sb([128, 16, 512], BF16, "ht")
    RS = P.sb([128, 128], F32, "rs")
    TMP = Rot([P.sb([128, 128], F32, f"tmp{i}") for i in range(4)])
    CQ = [P.sb([128, 515], F32, f"cq{i}") for i in range(3)]
    for t in CQ:
        P.op("dve", lambda e, t=t: e.memset(t.ap[:, 0:3], 0.0), [], [t])
    W512 = Rot([P.sb([128, 512], F32, f"w512_{i}") for i in range(6)])
    ZS = P.sb([128, 512], F32, "zs")
    QN = P.sb([128, 512], F32, "qn")
    KN = P.sb([128, 512], F32, "kn")
    VS = P.sb([128, 512], F32, "vs")
    QT = P.sb([128, 512], BF16, "qt")
    BA = P.sb([128, 8], F32, "ba")
    G4 = P.sb([128, 4], F32, "g4")
    BT4 = P.sb([128, 4], F32, "bt4")
    GC4 = P.sb([128, 4], F32, "gc4")
    NGC4 = P.sb([128, 4], F32, "ngc4")
    GL4 = P.sb([128, 4], F32, "gl4")
    BG4 = P.sb([128, 4], F32, "bg4")
    EDL4 = P.sb([128, 4], F32, "edl4")
    S128 = Rot([P.sb([128, 128], F32, f"s128_{i}") for i in range(10)])
    mk2 = lambda nm, dt: Rot([P.sb([128, 128], dt, f"{nm}{i}") for i in range(2)])
    R_KBG, R_VB, R_EGL, R_DEC, R_DECT, R_U = [mk2(nm, F32) for nm in ("kbg", "vb", "egl", "dec", "dect", "u")]
    R_KD, R_QGT, R_WTB, R_ATB, R_VN = [mk2(nm, BF16) for nm in ("kd", "qgt", "wtb", "atb", "vn")]
    SF = P.sb([128, 128], F32, "state")
    SBF = P.sb([128, 128], BF16, "stateb")
    P.op("dve", lambda e: e.memset(SF.ap[:], 0.0), [], [SF])
    P.op("dve", lambda e: e.memset(SBF.ap[:], 0.0), [], [SBF])
    OAB = P.sb([128, 512], BF16, "oab")
    OBB = P.sb([128, 512], BF16, "obb")
    EB = Rot([P.sb([128, 512], F32, f"eb{i}") for i in range(4)])
    SPB = Rot([P.sb([128, 512], F32, f"spb{i}") for i in range(3)])
    ECB = Rot([P.sb([128, 512], F32, f"ecb{i}") for i in range(2)])
    AB = Rot([P.sb([128, 512], BF16, f"ab{i}") for i in range(2)])
    RR = Rot([P.sb([1, 512], F32, f"rr{i}") for i in range(2)])

    CA = P.sb([128, 16], F32, "cact")
    P.act(CA[:], CV_[:], AF.Silu)
    WT = EB
    ACC = SPB.get()
    DWA = P.dram(wada, "wada")
    pm = small.get()
    for cg in range(8):
        for kc in range(16):
            w = WT.get()
            P.dma("sp", w[:], V(DWA, wada[kc * 128:(kc + 1) * 128, cg * 512:(cg + 1) * 512]))
            if kc == 0:
                P.ts(ACC[:], w[:], CA[:, 0:1], None, ALU.mult)
            else:
                P.stt(ACC[:], w[:], CA[:, kc:kc + 1], ACC[:], ALU.mult, ALU.add)
        for j in range(4):
            fc = cg * 4 + j
            P.mm(pm[:, fc:fc + 1], [(ACC[:, j * 128:(j + 1) * 128], ONE[:, 0:1])])
    P.tt(MODV[:], pm[:, 0:32], BAD[:], ALU.add)
    SH1 = lambda kc: MODV[:, kc:kc + 1]
    P.stt(G1[:], MODV[:, 16:32], 1.0, N1[:], ALU.add, ALU.mult)
    NEGA = P.sb([128, 1], F32, "nega")
    P.act(NEGA[:], PV[:, 0:1], AF.Exp)
    P.ts(NEGA[:], NEGA[:], -1.0, None, ALU.mult)
    QG_ = P.sb([128, 1], F32, "qgain")
    P.ts(QG_[:], PV[:, 3:4], float(128 ** -0.5), None, ALU.mult)

    if stage == 0:
        P.dump("modv", MODV[:], [128, 32], F32)
        P.dump("g1", G1[:], [128, 16], F32)
        NT = 0

    def rstd_from_ssq(out, ssq_ps, inv_n):
        P.act(out, ssq_ps, AF.Ln, bias=EPS, scale=inv_n)
        P.act(out, out, AF.Exp, scale=-0.5)

    def colnorm(out, src, inv_n, gain=None, tmpf=None):
        n = 512
        sq = W512.get()
        P.act(sq[:], src, AF.Square)
        ps = big.get()
        P.mm(ps[:], [(ONE, sq[:])])
        r = W512.get()
        rstd_from_ssq(r[:], ps[:], inv_n)
        if gain is None:
            P.tt(out, src, r[:], ALU.mult)
        else:
            P.stt(out, src, gain, r[:], ALU.mult, ALU.mult)

    for n in range(NT):
        t0 = n * 512
        for qd in range(4):
            c0 = t0 + qd * 128
            X = XR.get()
            r_, rem = divmod(c0, TB)
            j_, off = divmod(rem, TW)
            for g in range(4):
                P.dma("sp", X[:, 4 * g:4 * g + 4, :], V(DX, XS5[r_, j_, :, 4 * g:4 * g + 4, off:off + 128]))
            ps = big.get()
            for g in range(4):
                sq = SQ.get()
                P.act(sq[:], X[:, 4 * g:4 * g + 4, :], AF.Square)
                P.mm(ps[:, 0:128], [(ONE, sq[:, j, :]) for j in range(4)], start=(g == 0), stop=(g == 3))
            rstd_from_ssq(RS[:], ps[:, 0:128], 1.0 / D)
            for kc in range(16):
                tm = TMP.get()
                P.tt(tm[:], X[:, kc, :], RS[:], ALU.mult)
                P.act(HT[:, kc, qd * 128:(qd + 1) * 128], tm[:], AF.Identity, bias=SH1(kc), scale=G1[:, kc:kc + 1])
        def proj_fm(gi):
            ps = big.get()
            P.mm(ps[:], [(WB[:, kc, gi * 128:(gi + 1) * 128], HT[:, kc, :]) for kc in range(16)])
            return ps
        for gi in range(3):
            ps = proj_fm(gi)
            P.copy(CQ[gi][:, 3:515], ps[:], eng="act" if gi == 1 else "dve")
        ps = proj_fm(3)
        P.act(ZS[:], ps[:], AF.Silu)
        ps = proj_fm(4)
        qraw = W512.get()
        P.copy(qraw[:], ps[:], eng="act")
        qf = W512.get()
        colnorm(qf[:], qraw[:], 1.0 / 128, gain=QG_[:, 0:1])
        P.copy(QT[:], qf[:])
        ps = proj_fm(5)
        kraw = W512.get()
        P.copy(kraw[:], ps[:], eng="act")
        kf = W512.get()
        colnorm(kf[:], kraw[:], 1.0 / 128, gain=PV[:, 4:5])
        P.copy(KT[:, t0:t0 + 512], kf[:])
        ps = big.get()
        for b in range(4):
            P.mm(ps[:, b * 128:(b + 1) * 128],
                 [(HT[:, kc, b * 128:(b + 1) * 128], WB[:, kc, 768:896]) for kc in range(16)])
        P.copy(VA[:, 4 * n:4 * n + 4, :], V(ps, ps.ap[:, :].rearrange("p (b d) -> p b d", d=128)), eng="act")
        pb = small.get()
        for b in range(4):
            P.mm(pb[:, 2 * b:2 * b + 2],
                 [(HT[:, kc, b * 128:(b + 1) * 128], WB[:, kc, 896:898]) for kc in range(16)])
        P.copy(BA[:], pb[:, 0:8])
        BAv = BA.ap[:, :].rearrange("p (b two) -> p b two", two=2)
        if stage == 1:
            P.dump("ht", HT[:, 0, :], [128, 512], BF16)
            P.dump("ht15", HT[:, 15, :], [128, 512], BF16)
            P.dump("cq0", CQ[0][:], [128, 515], F32)
            P.dump("cq2", CQ[2][:], [128, 515], F32)
            P.dump("zs", ZS[:], [128, 512], F32)
            P.dump("qt", QT[:], [128, 512], BF16)
            P.dump("kt", KT[:, 0:512], [128, 512], BF16)
            P.dump("va", VA[:, 0, :], [128, 128], BF16)
            P.dump("ba", BA[:], [128, 8], F32)
            break
        def conv_silu(gi, out):
            src = CQ[gi]
            acc = W512.get()
            P.ts(acc[:], src[:, 3:515], CW[:, gi * 4 + 3:gi * 4 + 4], None, ALU.mult)
            for i in range(3):
                P.stt(acc[:], src[:, i:i + 512], CW[:, gi * 4 + i:gi * 4 + i + 1], acc[:], ALU.mult, ALU.add)
            P.act(out, acc[:], AF.Silu)
            P.copy(src[:, 0:3], src[:, 512:515], eng="act")
        qs = W512.get()
        conv_silu(0, qs[:])
        colnorm(QN[:], qs[:], 1.0, gain=float(128 ** -0.5))
        ks = W512.get()
        conv_silu(1, ks[:])
        colnorm(KN[:], ks[:], 1.0)
        conv_silu(2, VS[:])
        P.act(BT4[:], V(BA, BAv[:, :, 0]), AF.Sigmoid)
        e1 = S128.get()
        P.act(e1[:, 0:4], V(BA, BAv[:, :, 1]), AF.Exp, bias=PV[:, 1:2])
        P.act(e1[:, 0:4], e1[:, 0:4], AF.Ln, bias=1.0)
        P.ts(G4[:], e1[:, 0:4], NEGA[:, 0:1], None, ALU.mult)
        pg = small.get()
        P.mm(pg[:, 0:4], [(BUI, G4[:])])
        P.copy(GC4[:], pg[:, 0:4])
        P.ts(NGC4[:], pg[:, 0:4], -1.0, None, ALU.mult)
        pg2 = small.get()
        P.mm(pg2[:, 0:4], [(BDM, G4[:])])
        P.copy(GL4[:], pg2[:, 0:4])
        eg = S128.get()
        P.act(eg[:, 0:4], GC4[:], AF.Exp)
        P.tt(BG4[:], BT4[:], eg[:, 0:4], ALU.mult)
        dl = S128.get()
        P.tt(dl[:, 0:4], GL4[:], GC4[:], ALU.subtract)
        P.act(EDL4[:], dl[:, 0:4], AF.Exp)

        if stage == 2:
            print("ops at stage2", P.total)
            P.dump("qn", QN[:], [128, 512], F32)
            P.dump("kn", KN[:], [128, 512], F32)
            P.dump("vs", VS[:], [128, 512], F32)
            P.dump("g4", G4[:], [128, 4], F32)
            P.dump("bt4", BT4[:], [128, 4], F32)
            P.dump("gc4", GC4[:], [128, 4], F32)
            P.dump("gl4", GL4[:], [128, 4], F32)
            break
        def gdn_stream():
            for b in range(4):
                cs = slice(b * 128, (b + 1) * 128)
                pk = small.get()
                P.mm(pk[:], [(KN[:, cs], IDN)])
                yield
                KBG = R_KBG.get()
                P.ts(KBG[:], pk[:], BG4[:, b:b + 1], None, ALU.mult)
                KD = R_KD.get()
                P.ts(KD[:], pk[:], EDL4[:, b:b + 1], None, ALU.mult)
                pv = small.get()
                P.mm(pv[:], [(VS[:, cs], IDN)])
                yield
                VB = R_VB.get()
                P.ts(VB[:], pv[:], BT4[:, b:b + 1], None, ALU.mult)
                GB = S128.get()
                P.ts(GB[:], ONE, G4[:, b:b + 1], None, ALU.mult)
                pgc = small.get()
                P.mm(pgc[:], [(GB[:], BUI)])
                pgl = small.get()
                P.mm(pgl[:, 0:2], [(GB[:], CI)])
                EGL = R_EGL.get()
                P.copy(EGL[:, 0:2], pgl[:, 0:2])
                P.act(EGL[:, 0:2], EGL[:, 0:2], AF.Exp)
                E1 = S128.get()
                P.stt(E1[:], pgc[:], -1.0, NLS, ALU.mult, ALU.add)
                DEC = R_DEC.get()
                P.act(DEC[:], E1[:], AF.Exp, bias=GC4[:, b:b + 1])
                E2 = S128.get()
                P.tt(E2[:], pgc[:], NUI, ALU.add)
                DECT = R_DECT.get()
                P.act(DECT[:], E2[:], AF.Exp, bias=NGC4[:, b:b + 1])
                EGB = S128.get()
                P.copy(EGB[:], pgc[:])
                P.act(EGB[:], EGB[:], AF.Exp)
                QGT = R_QGT.get()
                P.tt(QGT[:], QN[:, cs], EGB[:], ALU.mult)
                pkk = small.get()
                P.mm(pkk[:], [(KN[:, cs], KN[:, cs])])
                yield
                Lk = S128.get()
                P.stt(Lk[:], pkk[:], BT4[:, b:b + 1], DEC[:], ALU.mult, ALU.mult)
                pm_ = small.get()
                P.mm(pm_[:], [(Lk[:], IDN)])
                Mk = S128.get()
                P.copy(Mk[:], pm_[:])
                Pk = S128.get()
                P.stt(Pk[:], pm_[:], -1.0, IDN, ALU.mult, ALU.add)
                for k in range(5):
                    pl = small.get()
                    P.mm(pl[:], [(Mk[:], Lk[:])])
                    Ln_ = S128.get()
                    P.copy(Ln_[:], pl[:])
                    if k < 4:
                        pmm = small.get()
                        P.mm(pmm[:], [(Lk[:], Mk[:])])
                        Mn = S128.get()
                        P.copy(Mn[:], pmm[:])
                    pp = small.get()
                    P.mm(pp[:], [(Ln_[:], Pk[:])])
                    Pn = S128.get()
                    P.tt(Pn[:], pp[:], Pk[:], ALU.add)
                    Lk, Pk = Ln_, Pn
                    if k < 4:
                        Mk = Mn
                TTm = Pk
                pu = small.get()
                P.mm(pu[:], [(TTm[:], VB[:])])
                U = R_U.get()
                P.copy(U[:], pu[:])
                pw = small.get()
                P.mm(pw[:], [(KBG[:], TTm[:])])
                WTb = R_WTB.get()
                P.copy(WTb[:], pw[:])
                pa = small.get()
                P.mm(pa[:], [(KN[:, cs], QN[:, cs])])
                ATb = R_ATB.get()
                P.tt(ATb[:], pa[:], DECT[:], ALU.mult)
                VN = R_VN.get()
                for c in range(2):
                    r = slice(64 * c, 64 * c + 64)
                    p1 = small.get()
                    P.mm(p1[:], [(WTb[:], SBF[:])])
                    yield
                    P.stt(VN[r, :], p1[r, :], -1.0, U[r, :], ALU.mult, ALU.add)
                    oc = slice(b * 128 + 64 * c, b * 128 + 64 * c + 64)
                    P.mm(OTP[:, oc], [(SBF[:], QGT[:, r]), (VN[r, :], ATb[r, r])])
                    yield
                    p3 = small.get()
                    P.mm(p3[:], [(KD[r, :], VN[r, :])])
                    yield
                    SD = S128.get()
                    P.act(SD[:], SF[:], AF.Copy, scale=EGL[:, c:c + 1])
                    P.tt(SF[:], p3[:], SD[:], ALU.add)
                    P.copy(SBF[:], SF[:], eng="act")
            O = W512.get()
            P.copy(O[:], OTP[:])
            On = W512.get()
            colnorm(On[:], O[:], 1.0 / 128)
            P.stt(OAB[:], On[:], PV[:, 2:3], ZS[:], ALU.mult, ALU.mult)
            for pc in range(512 // TW):
                dst_, col = divmod(t0 + pc * TW, TB)
                P.dma("pool", V(DO, OLOC[dst_ * 256:dst_ * 256 + 128, col:col + TW]), OAB[:, pc * TW:(pc + 1) * TW])

            yield
        nblk = 4 * n + 4
        order = list(range(nblk - 1, -1, -1))
        st = {}

        def stage_z(i):
            kb = order[i]
            zp = big.get()
            P.mm(zp[:], [(KT[:, kb * 128:(kb + 1) * 128], QT[:])])
            e = EB.get()
            P.act(e[:], zp[:], AF.Exp)
            sp_ = SPB.get()
            P.act(sp_[:], e[:], AF.Ln, bias=1.0)
            r = kb - 4 * n
            if r >= 0:
                mk = SBM[:, 384 - 128 * r:896 - 128 * r]
                P.tt(sp_[:], sp_[:], mk, ALU.mult)
                P.tt(e[:], e[:], mk, ALU.mult)
            st[i] = dict(e=e, sp=sp_)

        def stage_c(i):
            d = st[i]
            cp = big.get()
            pairs = [(UIN, d["sp"][:])]
            if i > 0:
                pairs.append((ONE[0:1, :], st[i - 1]["rr"][:]))
            P.mm(cp[:], pairs)
            rr = RR.get()
            P.copy(rr[:], cp[0:1, :], eng="act")
            ec = ECB.get()
            P.act(ec[:], cp[:], AF.Exp, scale=-1.0)
            d["rr"] = rr
            d["ec"] = ec

        def stage_av(i):
            d = st[i]
            kb = order[i]
            a = AB.get()
            P.tt(a[:], d["e"][:], d["ec"][:], ALU.mult)
            P.mm(AVP[:], [(VA[:, kb, :], a[:])], start=(i == 0), stop=(i == nblk - 1))
            if i >= 1:
                st.pop(i - 1, None)

        def sb_stream():
            for step in range(nblk + 2):
                if step < nblk:
                    stage_z(step)
                if 0 <= step - 1 < nblk:
                    stage_c(step - 1)
                if 0 <= step - 2 < nblk:
                    stage_av(step - 2)
                yield

        gs = gdn_stream()
        kq = max(1, -(-125 // (nblk + 2)))
        g_alive = True
        for _ in sb_stream():
            for _ in range(kq):
                if g_alive:
                    try:
                        next(gs)
                    except StopIteration:
                        g_alive = False
        while g_alive:
            try:
                next(gs)
            except StopIteration:
                g_alive = False
        P.copy(OBB[:], AVP[:], eng="act")
        for pc in range(512 // TW):
            dst_, col = divmod(t0 + pc * TW, TB)
            P.dma("pool", V(DO, OLOC[dst_ * 256 + 128:dst_ * 256 + 256, col:col + TW]), OBB[:, pc * TW:(pc + 1) * TW])

    P.phase_end()


def layout_vec(v, n):
    return np.ascontiguousarray(np.asarray(v, np.float32).reshape(n, 128).T)


def tile_x(xT, tw):
    Dd, Tt = xT.shape
    return np.ascontiguousarray(xT.reshape(16, 128, Tt // tw, tw).transpose(2, 1, 0, 3))


def untile_x(xt):
    n, p, kc, tw = xt.shape
    return np.ascontiguousarray(xt.transpose(2, 1, 0, 3).reshape(kc * p, n * tw))


NFC = DFF // 128


def phase_B(P, nc, Tn, l, XS2, DXS, OALL, DOA, XOUT, DXO, cst, cvec, hmask, stg):
    TB = Tn // NCORES
    TW = min(512, TB)
    NTL = TB // TW
    dT = lambda name, shape, dt, kind: nc.dram_tensor(f"{name}_{l}", list(shape), dt, kind=kind).ap()
    wada = dT("wada_b", [D, 8192], F32, "ExternalInput")
    bada = dT("bada_b", [128, 64], F32, "ExternalInput")
    n2 = dT("n2", [128, 16], F32, "ExternalInput")
    wout = dT("wout", [16, 128, 16 * 128], F32, "ExternalInput")
    wup = dT("wup", [2 * NFC, 128, 16 * 128], F32, "ExternalInput")
    wdn = dT("wdn", [16, 128, NFC * 128], F32, "ExternalInput")
    cwf = dT("cwf", [128, NFC * 3], F32, "ExternalInput")

    P.phase_begin()
    DX, DO_, DOUT = DXS, DOA, DXO
    DWO, DWU, DWD, DWA = P.dram(wout), P.dram(wup), P.dram(wdn), P.dram(wada)
    XO4 = XOUT.rearrange("(j p) (kc t) -> j p kc t", p=128, t=TW)
    xmine, DXM, xprev, DXP, omine, DOM, oprev, DOP = stg
    OA4 = OALL.rearrange("(h d r) t -> h d r t", h=8, d=8, r=256)
    P.dma("sp", V(DXM, xmine[:, :]), V(DXS, lambda pid: XS2[bass.ds(pid * (NTL * 128), NTL * 128), :]))
    P.dma("sp", V(DXP, xprev[:, :]),
          V(DXS, lambda pid: XS2[bass.ds((((pid + 7) % 8) * NTL + (NTL - 1)) * 128, 128), :]))
    P.dma("sp", V(DOM, omine.rearrange("(h r) t -> h r t", h=8)),
          V(DOA, lambda pid: OA4[:, bass.ds(pid, 1), :, :].rearrange("h o r t -> h (o r) t")))
    P.dma("sp", V(DOP, oprev.rearrange("(h r) t -> h r t", h=8)),
          V(DOA, lambda pid: OA4[:, bass.ds((pid + 7) % 8, 1), :, :].rearrange("h o r t -> h (o r) t")))
    XM4 = xmine.rearrange("(j p) (kc t) -> j p kc t", p=128, t=TW)
    XP3 = xprev.rearrange("p (kc t) -> p kc t", t=TW)
    OM3 = omine.rearrange("(kc p) t -> p kc t", p=128)
    OP3 = oprev.rearrange("(kc p) t -> p kc t", p=128)

    CST = P.sb([128, C_W], F32, "cst")
    CV_ = P.sb([128, 16], F32, "cvec")
    N2 = P.sb([128, 16], F32, "n2")
    BAD = P.sb([128, 64], F32, "bada")
    MODV = P.sb([128, 64], F32, "modv")
    G2 = P.sb([128, 16], F32, "g2")
    CWF = P.sb([128, NFC * 3], F32, "cwf")
    HM = P.sb([128, 1], F32, "hm")
    P.dma("sp", CST[:], V(P.dram(cst), cst[:, :]))
    P.dma("sp", CV_[:], V(P.dram(cvec), cvec[:, :]))
    P.dma("sp", N2[:], V(P.dram(n2), n2[:, :]))
    P.dma("sp", BAD[:], V(P.dram(bada), bada[:, :]))
    P.dma("sp", CWF[:], V(P.dram(cwf), cwf[:, :]))
    P.dma("sp", HM[:], V(P.dram(hmask), hmask[:, :]))
    ONE = CST[:, C_ONE:C_ONE + 128]

    big = Rot([P.ps([128, 512], F32, f"big{i}") for i in range(6)])
    hbk = Rot([P.ps([128, 512], F32, f"hb{i}") for i in range(2)])

    X = P.sb([128, 16, TW], F32, "x")
    OH = P.sb([128, 16, TW], BF16, "oh2")
    A = P.sb([128, NFC, TW], BF16, "a")
    Xh = P.sb([128, 16, 2], F32, "xh")
    OHh = P.sb([128, 16, 2], BF16, "ohh")
    HALO = P.sb([128, NFC, 2], F32, "halo")
    WS = Rot([P.sb([128, 16 * 128], BF16, f"ws{i}") for i in range(4)])
    WDS = Rot([P.sb([128, NFC * 128], BF16, f"wd{i}") for i in range(2)])
    SQ = Rot([P.sb([128, 4, TW], F32, f"sq{i}") for i in range(2)])
    RS = P.sb([128, TW], F32, "rs")
    RSh = P.sb([128, 2], F32, "rsh")
    SQh = P.sb([128, 16, 2], F32, "sqh")
    TMP = Rot([P.sb([128, TW], F32, f"tmp{i}") for i in range(3)])
    GP = Rot([P.sb([128, TW + 2], F32, f"gp{i}") for i in range(2)])
    CA_ = Rot([P.sb([128, TW], F32, f"cacc{i}") for i in range(2)])
    XO = Rot([P.sb([128, TW], F32, f"xo{i}") for i in range(2)])

    CA = P.sb([128, 16], F32, "cact")
    P.act(CA[:], CV_[:], AF.Silu)
    WT = Rot([P.sb([128, 512], F32, f"wt{i}") for i in range(3)])
    ACC = P.sb([128, 512], F32, "acc")
    pm = hbk.get()
    for cg in range(16):
        for kc in range(16):
            w = WT.get()
            P.dma("sp", w[:], V(DWA, wada[kc * 128:(kc + 1) * 128, cg * 512:(cg + 1) * 512]))
            if kc == 0:
                P.ts(ACC[:], w[:], CA[:, 0:1], None, ALU.mult)
            else:
                P.stt(ACC[:], w[:], CA[:, kc:kc + 1], ACC[:], ALU.mult, ALU.add)
        for j in range(4):
            fc = cg * 4 + j
            P.mm(pm[:, fc:fc + 1], [(ACC[:, j * 128:(j + 1) * 128], ONE[:, 0:1])])
    P.tt(MODV[:], pm[:, 0:64], BAD[:], ALU.add)
    GATE1 = lambda dc: MODV[:, dc:dc + 1]
    SH2 = lambda kc: MODV[:, 16 + kc:17 + kc]
    GATE2 = lambda dc: MODV[:, 48 + dc:49 + dc]
    P.stt(G2[:], MODV[:, 32:48], 1.0, N2[:], ALU.add, ALU.mult)

    for kc in range(16):
        P.dma("sp", Xh[:, kc, :], V(DXP, XP3[:, kc, TW - 2:TW]), allow_slow_non_contiguous=True)
    OHw = P.sb([128, 16, 8], BF16, "ohw")
    for kc in range(16):
        P.dma("sp", OHw[:, kc, :], V(DOP, OP3[:, kc, TB - 8:TB]), allow_slow_non_contiguous=True)
    P.copy(OHh[:], OHw[:, :, 6:8])

    def rstd_from_ssq(out, ssq_ps):
        P.act(out, ssq_ps, AF.Ln, bias=EPS, scale=1.0 / D)
        P.act(out, out, AF.Exp, scale=-0.5)

    for j in range(NTL):
        c0 = j * TW
        halo = (j == 0)
        for g in range(4):
            P.dma("sp", X[:, 4 * g:4 * g + 4, :], V(DXM, XM4[j, :, 4 * g:4 * g + 4, :]))
        for g in range(2):
            P.dma("sp", OH[:, 8 * g:8 * g + 8, :], V(DOM, OM3[:, 8 * g:8 * g + 8, c0:c0 + TW]))
        for dc in range(16):
            w = WS.get()
            P.dma("pool", w[:], V(DWO, wout[dc, :, :]))
            ps = big.get()
            P.mm(ps[:, 0:TW], [(w[:, kc * 128:(kc + 1) * 128], OH[:, kc, :]) for kc in range(16)])
            if halo:
                ph = hbk.get()
                P.mm(ph[:, 0:2], [(w[:, kc * 128:(kc + 1) * 128], OHh[:, kc, :]) for kc in range(16)])
                P.stt(Xh[:, dc, :], ph[:, 0:2], GATE1(dc), Xh[:, dc, :], ALU.mult, ALU.add)
            P.stt(X[:, dc, :], ps[:, 0:TW], GATE1(dc), X[:, dc, :], ALU.mult, ALU.add)
        ps = big.get()
        for g in range(4):
            sq = SQ.get()
            P.act(sq[:], X[:, 4 * g:4 * g + 4, :], AF.Square)
            P.mm(ps[:, 0:TW], [(ONE, sq[:, i, :]) for i in range(4)], start=(g == 0), stop=(g == 3))
        rstd_from_ssq(RS[:], ps[:, 0:TW])
        for kc in range(16):
            tm = TMP.get()
            P.tt(tm[:], X[:, kc, :], RS[:], ALU.mult)
            P.act(OH[:, kc, :], tm[:], AF.Identity, bias=SH2(kc), scale=G2[:, kc:kc + 1])
        if halo:
            P.act(SQh[:], Xh[:], AF.Square)
            ph = hbk.get()
            P.mm(ph[:, 0:2], [(ONE, SQh[:, i, :]) for i in range(16)])
            rstd_from_ssq(RSh[:], ph[:, 0:2])
            for kc in range(16):
                tm = TMP.get()
                P.tt(tm[:, 0:2], Xh[:, kc, :], RSh[:], ALU.mult)
                P.act(OHh[:, kc, :], tm[:, 0:2], AF.Identity, bias=SH2(kc), scale=G2[:, kc:kc + 1])
        for fc in range(NFC):
            wg = WS.get()
            P.dma("pool", wg[:], V(DWU, wup[fc, :, :]))
            wv = WS.get()
            P.dma("pool", wv[:], V(DWU, wup[NFC + fc, :, :]))
            pg = big.get()
            P.mm(pg[:, 0:TW], [(wg[:, kc * 128:(kc + 1) * 128], OH[:, kc, :]) for kc in range(16)])
            if halo:
                ph = hbk.get()
                P.mm(ph[:, 0:2], [(wg[:, kc * 128:(kc + 1) * 128], OHh[:, kc, :]) for kc in range(16)])
                P.ts(HALO[:, fc, :], ph[:, 0:2], HM[:, 0:1], None, ALU.mult)
            pv = big.get()
            P.mm(pv[:, 0:TW], [(wv[:, kc * 128:(kc + 1) * 128], OH[:, kc, :]) for kc in range(16)])
            gp = GP.get()
            P.copy(gp[:, 2:TW + 2], pg[:, 0:TW], eng="act")
            P.copy(gp[:, 0:2], HALO[:, fc, :])
            acc = CA_.get()
            P.ts(acc[:], gp[:, 2:TW + 2], CWF[:, fc * 3 + 2:fc * 3 + 3], None, ALU.mult)
            P.stt(acc[:], gp[:, 1:TW + 1], CWF[:, fc * 3 + 1:fc * 3 + 2], acc[:], ALU.mult, ALU.add)
            P.stt(acc[:], gp[:, 0:TW], CWF[:, fc * 3 + 0:fc * 3 + 1], acc[:], ALU.mult, ALU.add)
            P.copy(HALO[:, fc, :], gp[:, TW:TW + 2])
            P.act(acc[:], acc[:], AF.Silu)
            P.tt(A[:, fc, :], pv[:, 0:TW], acc[:], ALU.mult)
        for dc in range(16):
            w = WDS.get()
            P.dma("pool", w[:], V(DWD, wdn[dc, :, :]))
            ps = big.get()
            P.mm(ps[:, 0:TW], [(w[:, fk * 128:(fk + 1) * 128], A[:, fk, :]) for fk in range(NFC)])
            o = XO.get()
            P.stt(o[:], ps[:, 0:TW], GATE2(dc), X[:, dc, :], ALU.mult, ALU.add)
            P.dma("sp", V(DOUT, XO4[j, :, dc, :]), o[:])
    P.phase_end()


def build_fused(Tn, nlayers=2):
    TB = Tn // NCORES
    TW = min(512, TB)
    NTL = TB // TW
    nc = bass.Bass("TRN2", target_bir_lowering=False)
    x_t = nc.dram_tensor("x_t", [8 * NTL * 128, 16 * TW], F32, kind="ExternalInput").ap()
    cst = nc.dram_tensor("cst", [128, C_W], F32, kind="ExternalInput").ap()
    sbm = nc.dram_tensor("sbm", [128, 896], BF16, kind="ExternalInput").ap()
    cvec = nc.dram_tensor("cvec", [128, 16], F32, kind="ExternalInput").ap()
    hmask = nc.dram_tensor("hmask", [128, 1], F32, kind="ExternalInput").ap()
    xo = nc.dram_tensor("xo", [NTL * 128, 16 * TW], F32, kind="ExternalOutput").ap()
    oloc = nc.dram_tensor("oloc", [8 * 256, TB], BF16).ap()
    oall = nc.dram_tensor("oall", [64 * 256, TB], BF16).ap()
    xloc = nc.dram_tensor("xloc", [NTL * 128, 16 * TW], F32).ap()
    xall = nc.dram_tensor("xall", [8 * NTL * 128, 16 * TW], F32).ap()
    xmine = nc.dram_tensor("xmine", [NTL * 128, 16 * TW], F32).ap()
    xprev = nc.dram_tensor("xprev", [128, 16 * TW], F32).ap()
    omine = nc.dram_tensor("omine", [8 * 256, TB], BF16).ap()
    oprev = nc.dram_tensor("oprev", [8 * 256, TB], BF16).ap()
    P = Prog(nc)
    stg = (xmine, P.dram(xmine, "xmine"), xprev, P.dram(xprev, "xprev"),
           omine, P.dram(omine, "omine"), oprev, P.dram(oprev, "oprev"))
    DXIN, DXO = P.dram(x_t, "x_t"), P.dram(xo, "xo")
    DOL, DOA, DXL, DXA = P.dram(oloc, "oloc"), P.dram(oall, "oall"), P.dram(xloc, "xloc"), P.dram(xall, "xall")
    r5 = lambda ap: ap.rearrange("(r j p) (kc t) -> r j p kc t", r=8, p=128, t=TW)
    XS5, XS2, DXS = r5(x_t), x_t, DXIN
    for l in range(nlayers):
        phase_A(P, nc, Tn, l, XS5, DXS, oloc, DOL, cst, sbm, cvec)
        P.phase_begin()
        P.coll("AllGather", V(DOA, oall), V(DOL, oloc))
        P.phase_end()
        last = (l == nlayers - 1)
        phase_B(P, nc, Tn, l, XS2, DXS, oall, DOA, xo if last else xloc, DXO if last else DXL, cst, cvec, hmask, stg)
        if not last:
            P.phase_begin()
            P.coll("AllGather", V(DXA, xall), V(DXL, xloc))
            P.phase_end()
            XS5, XS2, DXS = r5(xall), xall, DXA
    P.finish()
    return nc


def host_inputs(inputs, Tn, nlayers=2):
    TB = Tn // NCORES
    TW = min(512, TB)
    NTL = TB // TW
    x = np.asarray(inputs["x"], np.float32)[0, :Tn]
    xT = np.ascontiguousarray(x.T)
    x_t = tile_x(xT, TW).reshape(8 * NTL * 128, 16 * TW)
    common = {"x_t": x_t, "cst": make_consts(), "sbm": make_sbmask(), "cvec": layout_vec(inputs["c"][0], 16)}
    perm = np.concatenate([np.r_[h * 128:(h + 1) * 128, 1024 + h * 128:1024 + (h + 1) * 128] for h in range(8)])
    for l in range(nlayers):
        w_out = np.asarray(inputs["w_out"][l], np.float32)[perm]
        w_up = np.asarray(inputs["w_up"][l], np.float32)
        w_dn = np.asarray(inputs["w_down"][l], np.float32)
        common[f"wout_{l}"] = np.ascontiguousarray(w_out.reshape(16, 128, 16, 128).transpose(2, 1, 0, 3).reshape(16, 128, 16 * 128))
        common[f"wup_{l}"] = np.ascontiguousarray(w_up.reshape(16, 128, 2 * NFC, 128).transpose(2, 1, 0, 3).reshape(2 * NFC, 128, 16 * 128))
        common[f"wdn_{l}"] = np.ascontiguousarray(w_dn.reshape(NFC, 128, 16, 128).transpose(2, 1, 0, 3).reshape(16, 128, NFC * 128))
        cf = np.asarray(inputs["conv_ffn"][l], np.float32)
        common[f"cwf_{l}"] = np.ascontiguousarray(cf.reshape(3, NFC, 128).transpose(2, 1, 0).reshape(128, NFC * 3))
        common[f"wada_a_{l}"] = np.ascontiguousarray(inputs["w_ada"][l][:, 0:4096])
        common[f"wada_b_{l}"] = np.ascontiguousarray(inputs["w_ada"][l][:, 4096:])
        common[f"bada_a_{l}"] = layout_vec(inputs["b_ada"][l][0:4096], 32)
        common[f"bada_b_{l}"] = layout_vec(inputs["b_ada"][l][4096:], 64)
        common[f"n1_{l}"] = layout_vec(inputs["norm1"][l], 16)
        common[f"n2_{l}"] = layout_vec(inputs["norm2"][l], 16)
    maps = []
    for h in range(NCORES):
        m = dict(common)
        m["hmask"] = np.full((128, 1), 0.0 if h == 0 else 1.0, np.float32)
        hs = slice(h * 128, (h + 1) * 128)
        for l in range(nlayers):
            w_in = np.asarray(inputs["w_in"][l], np.float32)
            conv = np.asarray(inputs["conv_qkv"][l], np.float32)
            m[f"win_{l}"] = np.ascontiguousarray(np.concatenate([
                w_in[:, 0:1024][:, hs], w_in[:, 1024:2048][:, hs], w_in[:, 2048:3072][:, hs], w_in[:, 3072:4096][:, hs],
                w_in[:, 4112:5136][:, hs], w_in[:, 5136:6160][:, hs], w_in[:, 6160:7184][:, hs],
                w_in[:, 4096 + h:4097 + h], w_in[:, 4104 + h:4105 + h]], axis=1))
            cw = np.zeros((128, 12), np.float32)
            for gi in range(3):
                cw[:, gi * 4:(gi + 1) * 4] = conv[:, gi * 1024 + h * 128: gi * 1024 + (h + 1) * 128].T
            m[f"convw_{l}"] = cw
            pv = np.zeros((128, 8), np.float32)
            pv[:, 0] = inputs["a_log"][l][h]
            pv[:, 1] = inputs["dt_bias"][l][h]
            pv[:, 2] = inputs["gdn_norm"][l]
            pv[:, 3] = inputs["sb_q_norm"][l]
            pv[:, 4] = inputs["sb_k_norm"][l]
            m[f"pvec_{l}"] = pv
        maps.append(m)
    return maps


_CACHE = {}


def run_fused(inputs, Tn, nlayers=2):
    key = ("F", Tn, nlayers)
    if key not in _CACHE:
        _CACHE[key] = build_fused(Tn, nlayers)
    nc = _CACHE[key]
    res = run_bass_kernel_spmd(nc, host_inputs(inputs, Tn, nlayers), core_ids=list(range(NCORES)))
    TB = Tn // NCORES
    TW = min(512, TB)
    NTL = TB // TW
    cols = [untile_x(np.asarray(res.results[i]["xo"]).reshape(NTL, 128, 16, TW)) for i in range(NCORES)]
    return np.concatenate(cols, axis=1)


def kernel(**inputs):
    inputs = {k: np.asarray(v) for k, v in inputs.items()}
    Tn = inputs["x"].shape[1]
    xT = run_fused(inputs, Tn)
    return np.ascontiguousarray(xT.T)[None].astype(np.float32)
```
